# Optimizing a Trainium2 kernel written in Bass

```python
import jax, jax.numpy as jnp
from jax import lax
import numpy as np

D_MODEL = 1024
BATCH = 4
SEQ = 8192
DEPTH = 2

GRID_W = 64
CTX_LEN = 256
D_MIX = 2 * D_MODEL
EPS = 1e-6

M_HEADS = 8
M_DV = D_MODEL // 8
M_DQK = M_DV // 2
M_WIDTH = M_HEADS * M_DV
QK_W = M_HEADS * M_DQK
M_CHUNK = 128
N_GATES = 4 * M_HEADS

S_WIDTH = D_MODEL // 2
S_GROUPS = 4
S_GC = S_WIDTH // S_GROUPS
S_CHUNK = 128
ROWS_PER_CHUNK = S_CHUNK // GRID_W

F_WIDTH = D_MODEL // 2
F_GROUPS = 4
F_GC = F_WIDTH // F_GROUPS

A_SPLITS = (QK_W, QK_W, M_WIDTH, N_GATES)
REST_SPLITS = (M_WIDTH, M_WIDTH, S_WIDTH, S_WIDTH, S_WIDTH, F_WIDTH, F_WIDTH)
A_IN = int(sum(A_SPLITS))
P_IN = A_IN + int(sum(REST_SPLITS))
A_SPLIT_IDX = tuple(int(i) for i in np.cumsum(A_SPLITS)[:-1])
REST_SPLIT_IDX = tuple(int(i) for i in np.cumsum(REST_SPLITS)[:-1])

kernel_name = 'hybrid_mlstm_sgu_fourier_dit_block'


def rms_norm(x, g):
    xf = x.astype(jnp.float32)
    y = xf * lax.rsqrt(jnp.mean(xf * xf, axis=-1, keepdims=True) + EPS)
    return (y * g.astype(jnp.float32)).astype(x.dtype)


def zero_state(bsz):
    return (jnp.zeros((bsz, M_HEADS, M_DQK, M_DV), jnp.float32),
            jnp.zeros((bsz, M_HEADS, M_DQK), jnp.float32),
            jnp.zeros((bsz, M_HEADS), jnp.float32))


def mlstm_scan(q, k, v, ig, lf, state):
    bsz, nh, t = ig.shape
    nc = t // M_CHUNK

    def chunks(a):
        return jnp.moveaxis(a.reshape(bsz, nh, nc, M_CHUNK, *a.shape[3:]), 2, 0)

    lower = jnp.tril(jnp.ones((M_CHUNK, M_CHUNK), dtype=bool))

    def step(carry, inp):
        C, n, m = carry
        qc, kc, vc, igc, lfc = inp
        b = jnp.cumsum(lfc, axis=-1)
        d_log = jnp.where(lower, b[..., :, None] - b[..., None, :] + igc[..., None, :], -jnp.inf)
        inter = b + m[..., None]
        m_t = jnp.maximum(inter, jnp.max(d_log, axis=-1))
        s = jnp.einsum('bhtk,bhsk->bhts', qc, kc) * jnp.exp(d_log - m_t[..., None])
        w_inter = jnp.exp(inter - m_t)
        num = jnp.einsum('bhts,bhsv->bhtv', s, vc) + w_inter[..., None] * jnp.einsum('bhtk,bhkv->bhtv', qc, C)
        den = jnp.sum(s, axis=-1) + w_inter * jnp.einsum('bhtk,bhk->bht', qc, n)
        hc = num / jnp.maximum(jnp.abs(den), jnp.exp(-m_t))[..., None]
        b_last = b[..., -1]
        w_log = b_last[..., None] - b + igc
        m_new = jnp.maximum(b_last + m, jnp.max(w_log, axis=-1))
        decay = jnp.exp(b_last + m - m_new)
        w_s = jnp.exp(w_log - m_new[..., None])
        C_new = decay[..., None, None] * C + jnp.einsum('bhs,bhsk,bhsv->bhkv', w_s, kc, vc)
        n_new = decay[..., None] * n + jnp.einsum('bhs,bhsk->bhk', w_s, kc)
        return (C_new, n_new, m_new), hc

    state, h = lax.scan(step, state, (chunks(q), chunks(k), chunks(v), chunks(ig), chunks(lf)))
    h = jnp.moveaxis(h, 0, 2).reshape(bsz, nh, t, M_DV)
    return h, state


def mlstm_from_proj(pa, b_gate, g_hnorm, init):
    bsz, t = pa.shape[:2]
    q, k, v, gates = jnp.split(pa, A_SPLIT_IDX, axis=-1)

    def heads(a, d):
        return a.astype(jnp.float32).reshape(bsz, t, M_HEADS, d).transpose(0, 2, 1, 3)

    qh = heads(q, M_DQK) * (M_DQK ** -0.5)
    kh = heads(k, M_DQK)
    vh = heads(v, M_DV)
    g = (gates.astype(jnp.float32) + b_gate.astype(jnp.float32)).reshape(bsz, t, 4, M_HEADS).transpose(2, 0, 3, 1)
    ig_f, lf_f = g[0], jax.nn.log_sigmoid(g[1])
    ig_b, lf_b = g[2], jax.nn.log_sigmoid(g[3])
    init_f, init_b = init
    h_f, st_f = mlstm_scan(qh, kh, vh, ig_f, lf_f, init_f)

    def rev(a):
        return jnp.flip(a, axis=2)

    h_b, st_b = mlstm_scan(rev(qh), rev(kh), rev(vh), rev(ig_b), rev(lf_b), init_b)
    h = h_f + rev(h_b)
    h = h * lax.rsqrt(jnp.mean(h * h, axis=-1, keepdims=True) + EPS)
    h = h * g_hnorm.astype(jnp.float32).reshape(M_HEADS, 1, M_DV)
    return h.transpose(0, 2, 1, 3).reshape(bsz, t, M_WIDTH), (st_f, st_b)


def spatial_gating(u, vs, g_sgu, w_sp, b_sp, n_chunks):
    bsz, t = u.shape[:2]
    vn = rms_norm(vs, g_sgu).reshape(bsz, n_chunks, S_CHUNK, S_GROUPS, S_GC)
    mixed = jnp.einsum('gts,bnsgc->bntgc', w_sp, vn) + b_sp.T[:, :, None]
    return u * mixed.reshape(bsz, t, S_WIDTH)


def fourier_mix(f, w_fno, b_fno):
    bsz, t = f.shape[:2]
    fg = f.astype(jnp.float32).reshape(bsz, t, F_GROUPS, F_GC)
    y = jnp.real(jnp.fft.fft2(fg, axes=(1, 3), norm='ortho'))
    y = jnp.einsum('btgc,gcd->btgd', y, w_fno.astype(jnp.float32)) + b_fno.astype(jnp.float32)
    return y.reshape(bsz, t, F_WIDTH).astype(f.dtype)


def mixer(h, n_chunks, init, w_in, b_gate, g_hnorm, g_sgu, w_sp, b_sp, w_fno, b_fno, w_out):
    p = h @ w_in
    h_a, states = mlstm_from_proj(p[..., :A_IN], b_gate, g_hnorm, init)
    o, z_a, u, vs, z_b, f, z_c = jnp.split(p[..., A_IN:], REST_SPLIT_IDX, axis=-1)
    y_a = h_a.astype(h.dtype) * jax.nn.sigmoid(o) * jax.nn.silu(z_a)
    y_b = spatial_gating(u, vs, g_sgu, w_sp, b_sp, n_chunks) * jax.nn.silu(z_b)
    y_c = fourier_mix(f, w_fno, b_fno) * jax.nn.silu(z_c)
    return jnp.concatenate([y_a, y_b, y_c], axis=-1) @ w_out, states


def setup_inputs(seed: int = 0) -> dict:
    key = jax.random.key(seed)
    ks = jax.random.split(key, 20)

    def nrm(k, shape, s):
        return jax.random.normal(k, shape, jnp.float32) * s

    x = nrm(ks[0], (BATCH, SEQ, D_MODEL), 1.0)
    c = nrm(ks[1], (BATCH, D_MODEL), 1.0)
    ctx = nrm(ks[2], (BATCH, CTX_LEN, D_MODEL), 1.0)
    c_ctx = nrm(ks[3], (D_MODEL,), 1.0)
    w_mod = nrm(ks[4], (DEPTH, D_MODEL, 3 * D_MODEL), 0.5 * D_MODEL ** -0.5)
    b_mod = nrm(ks[5], (DEPTH, 3 * D_MODEL), 0.01)
    g_pre = 1.0 + nrm(ks[6], (DEPTH, D_MODEL), 0.01)
    g_post = 1.0 + nrm(ks[7], (DEPTH, D_MODEL), 0.01)
    w_in = nrm(ks[8], (DEPTH, D_MODEL, P_IN), D_MODEL ** -0.5)
    ig_bias = nrm(ks[9], (DEPTH, 2, M_HEADS), 0.1)
    fg_bias = jnp.linspace(3.0, 6.0, M_HEADS) + nrm(ks[10], (DEPTH, 2, M_HEADS), 0.1)
    b_gate = jnp.stack([ig_bias, fg_bias], axis=2).reshape(DEPTH, N_GATES)
    g_hnorm = 1.0 + nrm(ks[11], (DEPTH, M_WIDTH), 0.01)
    g_sgu = 1.0 + nrm(ks[12], (DEPTH, S_WIDTH), 0.01)
    w_sp = nrm(ks[13], (DEPTH, S_GROUPS, S_CHUNK, S_CHUNK), S_CHUNK ** -0.5)
    b_sp = 1.0 + nrm(ks[14], (DEPTH, S_GROUPS, S_CHUNK), 0.01)
    w_fno = nrm(ks[15], (DEPTH, F_GROUPS, F_GC, F_GC), F_GC ** -0.5)
    b_fno = nrm(ks[16], (DEPTH, F_GROUPS, F_GC), 0.01)
    w_out = nrm(ks[17], (DEPTH, D_MIX, D_MODEL), D_MIX ** -0.5)
    return {'x': x, 'c': c, 'ctx': ctx, 'c_ctx': c_ctx, 'w_mod': w_mod, 'b_mod': b_mod,
            'g_pre': g_pre, 'g_post': g_post, 'w_in': w_in, 'b_gate': b_gate, 'g_hnorm': g_hnorm,
            'g_sgu': g_sgu, 'w_sp': w_sp, 'b_sp': b_sp, 'w_fno': w_fno, 'b_fno': b_fno, 'w_out': w_out}


def reference(x, c, ctx, c_ctx, w_mod, b_mod, g_pre, g_post, w_in, b_gate, g_hnorm,
              g_sgu, w_sp, b_sp, w_fno, b_fno, w_out):
    bsz = x.shape[0]
    rows = x.shape[1] // GRID_W
    lat_chunks = rows // ROWS_PER_CHUNK
    ctx_chunks = ctx.shape[1] // S_CHUNK
    xc = ctx
    for l in range(DEPTH):
        mod_lat = jax.nn.silu(c) @ w_mod[l] + b_mod[l]
        mod_ctx = jax.nn.silu(c_ctx) @ w_mod[l] + b_mod[l]
        sh_l, sc_l, gt_l = jnp.split(mod_lat, 3, axis=-1)
        sh_c, sc_c, gt_c = jnp.split(mod_ctx, 3, axis=-1)
        params = (w_in[l], b_gate[l], g_hnorm[l], g_sgu[l], w_sp[l], b_sp[l], w_fno[l], b_fno[l], w_out[l])
        hc = rms_norm(xc, g_pre[l]) * (1.0 + sc_c) + sh_c
        init0 = (zero_state(bsz), zero_state(bsz))
        if l < DEPTH - 1:
            yc, ctx_states = mixer(hc, ctx_chunks, init0, *params)
        else:
            _, ctx_states = mlstm_from_proj(hc @ w_in[l][:, :A_IN], b_gate[l], g_hnorm[l], init0)
        h = rms_norm(x, g_pre[l]) * (1.0 + sc_l[:, None, :]) + sh_l[:, None, :]
        y, _ = mixer(h, lat_chunks, ctx_states, *params)
        x = x + gt_l[:, None, :] * rms_norm(y, g_post[l])
        if l < DEPTH - 1:
            xc = xc + gt_c * rms_norm(yc, g_post[l])
    return x
```

```python
from contextlib import ExitStack
import numpy as np
import ml_dtypes
import concourse.bass as bass
import concourse.mybir as mybir
from concourse.bass_utils import run_bass_kernel_spmd

F32 = mybir.dt.float32
BF16 = mybir.dt.bfloat16
AF = mybir.ActivationFunctionType
ALU = mybir.AluOpType
AX = mybir.AxisListType

SAME_ENG_SYNC = True
DBG_NT = None
DBG_PH = (1, 2, 3)
DBG_STOP = None
NEG = -30000.0

D = 1024
T_LAT = 8192
T_CTX = 256
NT = (T_LAT + T_CTX) // 128
NTOK = T_LAT + T_CTX
NCOL = 3856
EPS = 1e-6


class Obj:
    __slots__ = ("name", "last_w", "readers", "dsem", "dcount")

    def __init__(self, name):
        self.name = name
        self.last_w = None
        self.readers = {}
        self.dsem = None
        self.dcount = 0


class T:
    def __init__(self, t, name, psum=False):
        self.t = t
        self.o = Obj(name)
        self.psum = psum

    def __getitem__(self, k):
        return self.t[k]


class Op:
    __slots__ = ("eng", "fn", "deps", "dma_obj", "dma_cnt", "signal", "seq", "extra")

    def __init__(self, eng, fn):
        self.eng = eng
        self.fn = fn
        self.deps = []
        self.dma_obj = None
        self.dma_cnt = 0
        self.signal = False
        self.seq = 0
        self.extra = []


ENGS = ("sp", "act", "pool", "pe", "dve")


class _Rec:
    def __getattr__(self, name):
        def f(*a, **k):
            self.call = (name, a, k)
        return f


class Prog:
    def __init__(self, nc):
        self.nc = nc
        self.ops = []
        self.base = ExitStack()
        self.stack = self.base
        self.dma_objs = []
        self.lastc = {}
        self.uid = 0
        self.enabled = True

    def phase(self):
        st = ExitStack()
        self.stack = st
        return st

    def sb(self, name, shape, dt):
        self.uid += 1
        nm = "%s_%d" % (name, self.uid)
        t = self.stack.enter_context(self.nc.sbuf_tensor(nm, list(shape), dt))
        return T(t, nm)

    def ps(self, name, shape=(128, 512), dt=F32):
        self.uid += 1
        nm = "%s_%d" % (name, self.uid)
        t = self.stack.enter_context(self.nc.psum_tensor(nm, list(shape), dt))
        return T(t, nm, psum=True)

    def add(self, eng, fn, reads=(), writes=(), semt=None):
        if not self.enabled:
            return None
        op = Op(eng, fn)
        deps = {}
        for t in reads:
            w = t.o.last_w
            if w is not None:
                deps[id(w)] = w
            if t.psum:
                for k_, r in t.o.readers.items():
                    if k_ != eng:
                        deps[id(r)] = r
        for t in writes:
            w = t.o.last_w
            if w is not None:
                deps[id(w)] = w
            for r in t.o.readers.values():
                deps[id(r)] = r
        is_dma = semt is not None
        for d in deps.values():
            if (not is_dma) and d.dma_obj is None and d.eng == eng:
                if eng == "pe" or not SAME_ENG_SYNC:
                    continue
            op.deps.append(d)
            d.signal = True
        if is_dma:
            o = semt.o
            if o.dsem is None:
                o.dsem = self.base.enter_context(self.nc.semaphore("d_" + o.name))
                self.dma_objs.append(o)
            o.dcount += 1
            op.dma_obj = o
            op.dma_cnt = o.dcount
        else:
            self.lastc[eng] = op
        for t in writes:
            t.o.last_w = op
            t.o.readers = {}
        wset = {id(t) for t in writes}
        for t in reads:
            if id(t) in wset:
                continue
            key = ("d", id(op.dma_obj)) if is_dma else eng
            t.o.readers[key] = op
        self.ops.append(op)
        return op

    def barrier(self):
        dma_tok = [(o, o.dcount) for o in self.dma_objs]
        lasts = list(self.lastc.values())
        for e in ENGS:
            op = Op(e, None)
            for d in lasts:
                op.deps.append(d)
                d.signal = True
            op.extra = list(dma_tok)
            self.ops.append(op)

    def dma(self, eng, out, in_, reads=(), writes=(), semt=None, **kw):
        return self.add(eng, lambda e: e.dma_start(out=out, in_=in_, **kw), reads, writes, semt=semt)

    def mm(self, out, lhsT, rhs, start, stop, reads=(), writes=(), **kw):
        return self.add("pe", lambda e: e.matmul(out, lhsT, rhs, start=start, stop=stop, **kw), reads, writes)

    def tr(self, out, in_, ident, reads=(), writes=()):
        return self.add("pe", lambda e: e.transpose(out, in_, ident), reads, writes)

    def act(self, out, in_, func, reads=(), writes=(), **kw):
        return self.add("act", lambda e: e.activation(out, in_, func, **kw), reads, writes)

    def dve(self, fn, reads=(), writes=()):
        r = _Rec()
        fn(r)
        name, a, k = r.call
        return self.add("dve", lambda e: getattr(e, name)(*a, **k), reads, writes)

    def emit(self):
        nc = self.nc
        esem = {e: self.base.enter_context(nc.semaphore("e_" + e)) for e in ENGS}
        cnt = {e: 0 for e in ENGS}
        for op in self.ops:
            if op.dma_obj is None and op.fn is not None and op.signal:
                cnt[op.eng] += 1
                op.seq = cnt[op.eng]
        per = {e: [] for e in ENGS}
        for op in self.ops:
            per[op.eng].append(op)

        def token(d):
            if d.dma_obj is not None:
                return d.dma_obj.dsem, 16 * d.dma_cnt
            return esem[d.eng], d.seq

        def run(engh, ename):
            waited = {}
            for op in per[ename]:
                toks = [token(d) for d in op.deps]
                toks += [(o.dsem, 16 * c) for (o, c) in op.extra]
                for sem, val in toks:
                    k = id(sem)
                    if waited.get(k, 0) >= val:
                        continue
                    waited[k] = val
                    engh.wait_ge(sem, val)
                if op.fn is None:
                    continue
                ins = op.fn(engh)
                if op.dma_obj is not None:
                    ins.then_inc(op.dma_obj.dsem, 16)
                elif op.signal:
                    ins.then_inc(esem[ename], 1)

        with nc.Block() as block:
            @block.sync
            def _(e):
                run(e, "sp")

            @block.scalar
            def _(e):
                run(e, "act")

            @block.gpsimd
            def _(e):
                run(e, "pool")

            @block.tensor
            def _(e):
                run(e, "pe")

            @block.vector
            def _(e):
                run(e, "dve")
        self.base.close()
        return cnt


def _consts():
    c = {}
    c["ident"] = np.eye(128, dtype=np.float32)
    s = np.arange(128)
    U = (s[:, None] <= s[None, :]).astype(np.float32)
    c["tri"] = np.stack([U, U.T.copy()])
    mF = np.where(s[:, None] <= s[None, :], 0.0, NEG).astype(np.float32)
    mB = np.where(s[:, None] >= s[None, :], 0.0, NEG).astype(np.float32)
    c["mask"] = np.stack([mF, mB])
    sel = np.zeros((4, 4, 128), np.float32)
    for h in range(4):
        sel[h, h, :] = 1.0
    c["sel"] = sel
    r = np.arange(128, dtype=np.float64)
    th = 2 * np.pi * np.outer(r, r) / 128.0
    c["cs128"] = np.concatenate([np.cos(th), -np.sin(th)], axis=1).astype(np.float32)
    c["cc128"] = np.stack([np.cos(th), np.sin(th)]).astype(np.float32)
    col = np.arange(64, dtype=np.float64)
    k = np.arange(8192, dtype=np.float64)
    ph = 2 * np.pi * np.outer(col, k) / 8192.0
    co, si = np.cos(ph), np.sin(ph)
    wr = np.concatenate([co, si], axis=0)
    wi = np.concatenate([-si, co], axis=0)

    def perm(a):
        return np.ascontiguousarray(a.reshape(128, 64, 128).transpose(0, 2, 1))
    c["w3"] = np.stack([perm(wr), perm(wi)]).astype(np.float32)
    t = np.arange(256, dtype=np.float64)
    p2 = 2 * np.pi * np.outer(t, t) / 256.0
    c["cs256"] = np.stack([np.cos(p2), -np.sin(p2)]).astype(np.float32).reshape(2, 2, 128, 256)
    return c


def build_A():
    nc = bass.Bass("TRN2", target_bir_lowering=False)

    def din(name, shape, dt=F32):
        return nc.dram_tensor(name, list(shape), dt, kind="ExternalInput").ap()

    xin = din("xin", [NTOK, D])
    cvec = din("cvec", [2, D])
    w_mod = din("w_mod", [D, 2048])
    b_mod = din("b_mod", [2048])
    g_pre = din("g_pre", [D])
    w_in = din("w_in", [D, NCOL])
    b_gate = din("b_gate", [16])
    g_hn = din("g_hn", [512])
    g_sgu = din("g_sgu", [256])
    w_spT = din("w_spT", [2, 128, 128])
    b_sp = din("b_sp", [2, 128])
    w_fno = din("w_fno", [2, 128, 128])
    b_fno = din("b_fno", [2, 128])
    ident_d = din("ident", [128, 128])
    tri_d = din("tri", [2, 128, 128])
    mask_d = din("mask", [2, 128, 128])
    sel_d = din("sel", [4, 4, 128])
    cs128_d = din("cs128", [128, 256])
    cc128_d = din("cc128", [2, 128, 128])
    w3_d = din("w3", [2, 128, 128, 64])
    cs256_d = din("cs256", [2, 2, 128, 256])
    yT = nc.dram_tensor("yT", [1024, NTOK], BF16, kind="ExternalOutput").ap()

    def dint(name, shape, dt):
        return nc.dram_tensor(name, list(shape), dt, kind="Internal").ap()

    s_qT = dint("s_qT", [256, NTOK], BF16)
    s_kT = dint("s_kT", [256, NTOK], BF16)
    s_k = dint("s_k", [NTOK, 256], BF16)
    s_v = dint("s_v", [NTOK, 512], BF16)
    s_ga = dint("s_ga", [NTOK, 512], BF16)
    s_hf = dint("s_hf", [NTOK, 512], F32)
    s_f = dint("s_f", [NTOK, 256], BF16)
    s_zcT = dint("s_zcT", [256, NTOK], BF16)

    P = Prog(nc)
    ident_f = P.sb("ident_f", [128, 128], F32)
    ident_b = P.sb("ident_b", [128, 128], BF16)
    tri = P.sb("tri", [128, 2, 128], F32)
    mask_f = P.sb("mask_f", [128, 2, 128], F32)
    mask_b = P.sb("mask_b", [128, 2, 128], BF16)
    sel = P.sb("sel", [4, 4, 128], F32)
    sel_b = P.sb("sel_b", [4, 4, 128], BF16)
    tri_b = P.sb("tri_b", [128, 2, 128], BF16)
    LF3 = P.sb("LF3", [128, NT, 3, 8], BF16)
    r1 = P.sb("r1", [128, 8], F32)
    bT3 = P.sb("bT3", [4, 3, 128], BF16)
    rb = P.sb("rb", [4, 128], F32)
    modA = P.sb("modA", [128, 8, 2], F32)
    modS = P.sb("modS", [128, 8, 2], F32)
    bgate_bc = P.sb("bgate_bc", [128, 16], F32)
    ghn_bc = P.sb("ghn_bc", [128, 512], F32)
    gsgu_bc = P.sb("gsgu_bc", [128, 256], F32)
    wsp_b = P.sb("wsp_b", [128, 2, 128], BF16)
    bsp_t = P.sb("bsp_t", [128, 2], F32)
    IG = P.sb("IG", [128, NT, 8], F32)
    LF = P.sb("LF", [128, NT, 8], F32)
    Cm = P.sb("Cm", [128, 2, 129], F32)
    Cb = P.sb("Cb", [128, 2, 129], BF16)

    pT = P.ps("pT", [128, 1024], BF16)
    pP = [P.ps("pP0"), P.ps("pP1")]
    pF = P.ps("pF")
    pDS = P.ps("pDS")
    pNI = P.ps("pNI")
    pC = P.ps("pC")
    pX = P.ps("pX")

    def ld(dst, src, eng="sp", **kw):
        P.dma(eng, dst[:] if isinstance(dst, T) else dst, src, writes=[dst] if isinstance(dst, T) else [], semt=dst, **kw)

    ld(ident_f, ident_d)
    ld(tri, tri_d.rearrange("a s t -> s a t"))
    ld(mask_f, mask_d.rearrange("a s t -> s a t"))
    ld(sel, sel_d)
    ld(bgate_bc, b_gate.partition_broadcast(128))
    ld(ghn_bc, g_hn.partition_broadcast(128))
    ld(gsgu_bc, g_sgu.partition_broadcast(128))
    ld(bsp_t, b_sp.rearrange("g t -> t g"), allow_slow_non_contiguous=True)
    P.dve(lambda e: e.tensor_copy(ident_b[:], ident_f[:]), [ident_f], [ident_b])
    P.dve(lambda e: e.tensor_copy(mask_b[:], mask_f[:]), [mask_f], [mask_b])
    P.dve(lambda e: e.tensor_copy(tri_b[:], tri[:]), [tri], [tri_b])
    P.dve(lambda e: e.tensor_copy(sel_b[:], sel[:]), [sel], [sel_b])
    P.dve(lambda e: e.memset(Cm[:], 0.0), [], [Cm])
    P.dve(lambda e: e.memset(Cb[:], 0.0), [], [Cb])

    ph0 = P.phase()
    wmod = P.sb("wmod", [128, 8, 2048], F32)
    cT = P.sb("cT", [128, 8, 2], F32)
    scT = P.sb("scT", [128, 8, 2], F32)
    bmodT = P.sb("bmodT", [128, 16], F32)
    gpreT = P.sb("gpreT", [128, 8], F32)
    modT = P.sb("modT", [128, 16, 2], F32)
    wspf = P.sb("wspf", [128, 2, 128], F32)
    ld(wmod, w_mod.rearrange("(c p) n -> p c n", p=128))
    wmodb = P.sb("wmodb", [128, 8, 2048], BF16)
    scTb = P.sb("scTb", [128, 8, 2], BF16)
    for c in range(8):
        if c % 2 == 0:
            P.dve(lambda e: e.tensor_copy(wmodb[:, c, :], wmod[:, c, :]), [wmod], [wmodb])
        else:
            P.act(wmodb[:, c, :], wmod[:, c, :], AF.Copy, [wmod], [wmodb])
    for m in range(2):
        P.dma("sp", cT[:, :, m], cvec[m].rearrange("(c p) -> p c", p=128), writes=[cT], semt=cT, allow_slow_non_contiguous=True)
    ld(bmodT, b_mod.rearrange("(c p) -> p c", p=128), allow_slow_non_contiguous=True)
    ld(gpreT, g_pre.rearrange("(c p) -> p c", p=128), allow_slow_non_contiguous=True)
    ld(wspf, w_spT.rearrange("g s t -> s g t"))
    P.dve(lambda e: e.tensor_copy(wsp_b[:], wspf[:]), [wspf], [wsp_b])
    P.act(scT[:], cT[:], AF.Silu, [cT], [scT])
    P.dve(lambda e: e.tensor_copy(scTb[:], scT[:]), [scT], [scTb])
    for jb in range(16):
        for c in range(8):
            P.mm(pX[:, 2 * jb:2 * jb + 2], wmodb[:, c, jb * 128:(jb + 1) * 128], scTb[:, c, :], c == 0, c == 7,
                 [wmodb, scTb], [pX])
    for m in range(2):
        P.dve(lambda e: e.tensor_tensor(modT[:, :, m], pX[:, m:32:2], bmodT[:], ALU.add), [pX, bmodT], [modT])
        P.dve(lambda e: e.tensor_copy(modS[:, :, m], modT[:, 0:8, m]), [modT], [modS])
        P.dve(lambda e: e.scalar_tensor_tensor(modA[:, :, m], modT[:, 8:16, m], 1.0, gpreT[:], ALU.add, ALU.mult),
              [modT, gpreT], [modA])
    P.barrier()
    ph0.close()

    ph1 = P.phase()
    wb = P.sb("wb", [128, 8, NCOL], BF16)
    wst = [P.sb("wst0", [128, NCOL], F32), P.sb("wst1", [128, NCOL], F32)]
    for c in range(8):
        ld(wst[c % 2], w_in[c * 128:(c + 1) * 128, :])
        if c % 2 == 0:
            P.dve(lambda e, c=c: e.tensor_copy(wb[:, c, :], wst[c % 2][:]), [wst[c % 2]], [wb])
        else:
            P.act(wb[:, c, :], wst[c % 2][:], AF.Copy, [wst[c % 2]], [wb])

    xt = [P.sb("xt0", [128, D], F32), P.sb("xt1", [128, D], F32)]
    junk = P.sb("junk", [128, D], F32)
    ss = P.sb("ss", [128, 1], F32)
    rstd = P.sb("rstd", [128, 1], F32)
    xn = P.sb("xn", [128, D], BF16)
    hT = P.sb("hT", [128, 8, 128], BF16)
    ktok = [P.sb("ktok0", [128, 4, 64], BF16), P.sb("ktok1", [128, 4, 64], BF16)]
    vaug = [P.sb("vaug0", [128, 4, 129], BF16), P.sb("vaug1", [128, 4, 129], BF16)]
    qT = [P.sb("qT0", [128, 2, 128], BF16), P.sb("qT1", [128, 2, 128], BF16)]
    kT = [P.sb("kT0", [128, 2, 128], BF16), P.sb("kT1", [128, 2, 128], BF16)]
    g16 = P.sb("g16", [128, 16], F32)
    e16 = P.sb("e16", [128, 8], F32)
    so = P.sb("so", [128, 512], F32)
    sz = P.sb("sz", [128, 512], F32)
    ga = [P.sb("ga0", [128, 512], BF16), P.sb("ga1", [128, 512], BF16)]
    szb = P.sb("szb", [128, 256], F32)
    usb = P.sb("usb", [128, 256], F32)
    ssv = P.sb("ssv", [128, 1], F32)
    rstdv = P.sb("rstdv", [128, 1], F32)
    vso = P.sb("vso", [128, 256], F32)
    vnb = P.sb("vnb", [128, 256], BF16)
    mix = P.sb("mix", [128, 256], F32)
    yb = P.sb("yb", [128, 256], BF16)
    ybT = [P.sb("ybT0", [128, 2, 128], BF16), P.sb("ybT1", [128, 2, 128], BF16)]
    fb = [P.sb("fb0", [128, 256], BF16), P.sb("fb1", [128, 256], BF16)]
    zcT = [P.sb("zcT0", [128, 2, 128], BF16), P.sb("zcT1", [128, 2, 128], BF16)]
    hf = [P.sb("hf0", [128, 4, 128], F32), P.sb("hf1", [128, 4, 128], F32)]
    for v_ in vaug:
        P.dve(lambda e, v_=v_: e.memset(v_[:], 1.0), [], [v_])

    b_sb = P.sb("b_sb", [128, 4], F32)
    a_sb = P.sb("a_sb", [128, 4], F32)
    eb = P.sb("eb", [128, 4], F32)
    bT = P.sb("bT", [4, 128], F32)
    DT = P.sb("DT", [128, 2, 128], F32)
    Sp = P.sb("Sp", [128, 2, 128], BF16)
    tmpI = P.sb("tmpI", [128, 129], F32)
    tot = P.sb("tot", [128, 4, 129], F32)
    dn = P.sb("dn", [128, 4], F32)
    dec = P.sb("dec", [128, 4], F32)
    kw = P.sb("kw", [128, 64], BF16)

    def mlstm_tile(i, d, q_t, k_t, kt_t, va_t, h_out):
        lf = LF[:, i, 4 * d:4 * d + 4]
        ig = IG[:, i, 4 * d:4 * d + 4]
        tl = 127 if d == 0 else 0
        for pt in range(3):
            P.mm(pC[:, 320:324], tri_b[:, d, :], LF3[:, i, pt, 4 * d:4 * d + 4], pt == 0, pt == 2, [tri_b, LF3], [pC])
        for pt in range(3):
            P.mm(pC[0:4, 384:512], LF3[:, i, pt, 4 * d:4 * d + 4], tri_b[:, d, :], pt == 0, pt == 2, [tri_b, LF3], [pC])
        P.dve(lambda e: e.tensor_copy(b_sb[:], pC[:, 320:324]), [pC], [b_sb])
        P.dve(lambda e: e.tensor_copy(bT[:], pC[0:4, 384:512]), [pC], [bT])
        P.dve(lambda e: e.tensor_copy(bT3[:, 0, :], bT[:]), [bT], [bT3])
        P.dve(lambda e: e.tensor_tensor(rb[:], bT[:], bT3[:, 0, :], ALU.subtract), [bT, bT3], [rb])
        P.dve(lambda e: e.tensor_copy(bT3[:, 1, :], rb[:]), [rb], [bT3])
        P.dve(lambda e: e.tensor_tensor(rb[:], rb[:], bT3[:, 1, :], ALU.subtract), [rb, bT3], [rb])
        P.dve(lambda e: e.tensor_copy(bT3[:, 2, :], rb[:]), [rb], [bT3])
        P.dve(lambda e: e.tensor_tensor(a_sb[:], ig, b_sb[:], ALU.subtract), [IG, b_sb], [a_sb])
        P.act(eb[:], b_sb[:], AF.Exp, [b_sb], [eb])
        for hp in range(2):
            for j in range(2):
                h = 2 * hp + j
                for pt in range(3):
                    P.mm(pDS[:, j * 128:(j + 1) * 128], sel_b[:, h, :], bT3[:, pt, :], pt == 0, False, [sel_b, bT3], [pDS])
                P.mm(pDS[:, j * 128:(j + 1) * 128], ident_b[:], mask_b[:, d, :], False, True, [ident_b, mask_b], [pDS])
            for j in range(2):
                h = 2 * hp + j
                pr = slice(64 * j, 64 * j + 64)
                pS_ = pDS if j == 0 else pX
                P.mm(pS_[:, 256:384], k_t[pr, hp, :], q_t[pr, hp, :], True, True, [k_t, q_t], [pS_])
            for j in range(2):
                h = 2 * hp + j
                P.act(DT[:, j, :], pDS[:, j * 128:(j + 1) * 128], AF.Exp, [pDS, a_sb], [DT], bias=a_sb[:, h:h + 1])
            P.act(dec[:, 2 * hp:2 * hp + 2], pDS[:, tl:tl + 256:128], AF.Exp, [pDS], [dec])
            P.dve(lambda e: e.tensor_tensor(Sp[:, 0, :], pDS[:, 256:384], DT[:, 0, :], ALU.mult), [pDS, DT], [Sp])
            P.dve(lambda e: e.tensor_tensor(Sp[:, 1, :], pX[:, 256:384], DT[:, 1, :], ALU.mult), [pX, DT], [Sp])
            for j in range(2):
                h = 2 * hp + j
                pr = slice(64 * j, 64 * j + 64)
                P.mm(pNI[:, 0:129], Sp[:, j, :], va_t[:, h, :], True, True, [Sp, va_t], [pNI])
                P.mm(pNI[:, 192:321], q_t[pr, hp, :], Cb[pr, hp, :], True, True, [q_t, Cb], [pNI])
                P.act(tmpI[:], pNI[:, 192:321], AF.Copy, [pNI, eb], [tmpI], scale=eb[:, h:h + 1])
                P.dve(lambda e, h=h: e.tensor_tensor(tot[:, h, :], pNI[:, 0:129], tmpI[:], ALU.add), [pNI, tmpI], [tot])
                P.dve(lambda e, h=h, j=j: e.tensor_scalar(kw[:], kt_t[:, h, :], DT[:, j, tl:tl + 1], None, ALU.mult),
                      [kt_t, DT], [kw])
                P.mm(pC[pr, hp * 129:(hp + 1) * 129], kw[:], va_t[:, h, :], True, True, [kw, va_t], [pC])
                P.dve(lambda e, h=h, pr=pr, hp=hp: e.scalar_tensor_tensor(
                    Cm[pr, hp, :], Cm[pr, hp, :], dec[pr, h:h + 1], pC[pr, hp * 129:(hp + 1) * 129], ALU.mult, ALU.add),
                    [Cm, dec, pC], [Cm])
        P.act(Cb[:], Cm[:], AF.Copy, [Cm], [Cb])
        P.act(dn[:], tot[:, :, 128], AF.Abs, [tot], [dn])
        P.dve(lambda e: e.tensor_scalar_max(dn[:], dn[:], 1.0), [dn], [dn])
        P.dve(lambda e: e.reciprocal(dn[:], dn[:]), [dn], [dn])
        for h in range(4):
            P.dve(lambda e, h=h: e.tensor_scalar(h_out[:, h, :], tot[:, h, 0:128], dn[:, h:h + 1], None, ALU.mult),
                  [tot, dn], [h_out])


    CQ, CKT, CZC, CK, CG, CV, CO, CZA, CU, CZB, CVS, CF = 0, 256, 512, 768, 1024, 1040, 1552, 2064, 2576, 2832, 3088, 3600
    BLK = [(CK, 272), (CV, 512), (CO, 512), (CZA, 512), (CU, 512), (CVS, 512), (CF, 256)]
    st_eng = "pool"

    def load_x(i):
        P.dma("sp", xt[i % 2][:], xin[i * 128:(i + 1) * 128, :], writes=[xt[i % 2]], semt=xt[i % 2])

    load_x(0)
    order1 = list(range(NT if DBG_NT is None else DBG_NT))
    if 1 not in DBG_PH:
        order1 = []
    for i in order1:
        if i == 2:
            pass
        if i + 1 < len(order1):
            load_x(i + 1)
        sid = 0 if i < 2 else 1
        x_ = xt[i % 2]
        tok = slice(i * 128, (i + 1) * 128)
        P.act(junk[:], x_[:], AF.Square, [x_], [junk, ss], accum_out=ss[:])
        P.dve(lambda e: e.tensor_scalar(rstd[:], ss[:], 1.0 / D, EPS, ALU.mult, ALU.add), [ss], [rstd])
        P.act(rstd[:], rstd[:], AF.Sqrt, [rstd], [rstd])
        P.dve(lambda e: e.reciprocal(rstd[:], rstd[:]), [rstd], [rstd])
        P.act(xn[:], x_[:], AF.Copy, [x_, rstd], [xn], scale=rstd[:])
        for c in range(8):
            P.tr(pT[:, c * 128:(c + 1) * 128], xn[:, c * 128:(c + 1) * 128], ident_b[:], [xn, ident_b], [pT])
        for c in range(8):
            P.act(hT[:, c, :], pT[:, c * 128:(c + 1) * 128], AF.Identity, [pT, modA, modS], [hT],
                  scale=modA[:, c, sid:sid + 1], bias=modS[:, c, sid:sid + 1])
        q_t, k_t, kt_t, va_t = qT[i % 2], kT[i % 2], ktok[i % 2], vaug[i % 2]
        P.enabled = DBG_STOP is None or DBG_STOP >= 2
        for gi, c0 in enumerate((CQ, CQ + 128, CKT, CKT + 128)):
            for c in range(8):
                P.mm(pF[:, gi * 128:(gi + 1) * 128], wb[:, c, c0:c0 + 128], hT[:, c, :], c == 0, c == 7, [wb, hT], [pF])
        P.act(q_t[:], pF[:, 0:256].rearrange("p (a t) -> p a t", a=2), AF.Copy, [pF], [q_t], scale=0.125)
        P.dve(lambda e: e.tensor_copy(k_t[:], pF[:, 256:512].rearrange("p (a t) -> p a t", a=2)), [pF], [k_t])
        P.dma(st_eng, s_qT.rearrange("(a p) t -> p a t", p=128)[:, :, tok], q_t[:], reads=[q_t], semt=q_t)
        P.dma(st_eng, s_kT.rearrange("(a p) t -> p a t", p=128)[:, :, tok], k_t[:], reads=[k_t], semt=k_t)
        P.enabled = DBG_STOP is None or DBG_STOP >= 3
        for gi, c0 in enumerate((CZC, CZC + 128)):
            for c in range(8):
                P.mm(pF[:, gi * 128:(gi + 1) * 128], wb[:, c, c0:c0 + 128], hT[:, c, :], c == 0, c == 7, [wb, hT], [pF])
        z_t = zcT[i % 2]
        P.act(z_t[:], pF[:, 0:256].rearrange("p (a t) -> p a t", a=2), AF.Silu, [pF], [z_t])
        P.dma(st_eng, s_zcT.rearrange("(a p) t -> p a t", p=128)[:, :, tok], z_t[:], reads=[z_t], semt=z_t)
        P.enabled = DBG_STOP is None or DBG_STOP >= 4
        for bi, (c0, n) in enumerate(BLK):
            pp = pP[bi % 2]
            for c in range(8):
                P.mm(pp[:, 0:n], hT[:, c, :], wb[:, c, c0:c0 + n], c == 0, c == 7, [hT, wb], [pp])
            if bi == 0:
                P.dve(lambda e, pp=pp: e.tensor_copy(kt_t[:], pp[:, 0:256].rearrange("p (h k) -> p h k", h=4)), [pp], [kt_t])
                P.dve(lambda e, pp=pp: e.tensor_tensor(g16[:], pp[:, 256:272], bgate_bc[:], ALU.add), [pp, bgate_bc], [g16])
                P.dma(st_eng, s_k[tok, :], kt_t[:].rearrange("p h k -> p (h k)"), reads=[kt_t], semt=kt_t)
                P.dve(lambda e: e.tensor_copy(IG[:, i, 0:4], g16[:, 0:4]), [g16], [IG])
                P.dve(lambda e: e.tensor_copy(IG[:, i, 4:8], g16[:, 8:12]), [g16], [IG])
                P.act(e16[:, 0:4], g16[:, 4:8], AF.Exp, [g16], [e16], scale=-1.0)
                P.act(e16[:, 4:8], g16[:, 12:16], AF.Exp, [g16], [e16], scale=-1.0)
                P.act(e16[:], e16[:], AF.Ln, [e16], [e16], bias=1.0)
                P.dve(lambda e: e.tensor_scalar(LF[:, i, :], e16[:], -1.0, None, ALU.mult), [e16], [LF])
                P.dve(lambda e: e.tensor_copy(LF3[:, i, 0, :], LF[:, i, :]), [LF], [LF3])
                P.dve(lambda e: e.tensor_tensor(r1[:], LF[:, i, :], LF3[:, i, 0, :], ALU.subtract), [LF, LF3], [r1])
                P.dve(lambda e: e.tensor_copy(LF3[:, i, 1, :], r1[:]), [r1], [LF3])
                P.dve(lambda e: e.tensor_tensor(r1[:], r1[:], LF3[:, i, 1, :], ALU.subtract), [r1, LF3], [r1])
                P.dve(lambda e: e.tensor_copy(LF3[:, i, 2, :], r1[:]), [r1], [LF3])
            elif bi == 1:
                P.act(va_t[:, :, 0:128], pp[:, 0:512].rearrange("p (h v) -> p h v", h=4), AF.Copy, [pp], [va_t])
                P.dma(st_eng, s_v[tok, :].rearrange("p (h v) -> p h v", h=4), va_t[:, :, 0:128], reads=[va_t], semt=va_t)
            elif bi == 2:
                P.act(so[:], pp[:, 0:512], AF.Sigmoid, [pp], [so])
            elif bi == 3:
                P.act(sz[:], pp[:, 0:512], AF.Silu, [pp], [sz])
                g_ = ga[i % 2]
                P.dve(lambda e, g_=g_: e.tensor_tensor(g_[:], so[:], sz[:], ALU.mult), [so, sz], [g_])
                P.dma(st_eng, s_ga[tok, :], g_[:], reads=[g_], semt=g_)
            elif bi == 4:
                P.dve(lambda e, pp=pp: e.tensor_copy(usb[:], pp[:, 0:256]), [pp], [usb])
                P.act(szb[:], pp[:, 256:512], AF.Silu, [pp], [szb])
            elif bi == 5:
                P.act(junk[:, 0:512], pp[:, 0:512], AF.Square, [pp], [junk, ssv], accum_out=ssv[:])
                P.dve(lambda e, pp=pp: e.tensor_copy(vso[:], pp[:, 0:256]), [pp], [vso])
            else:
                f_ = fb[i % 2]
                P.act(f_[:], pp[:, 0:256], AF.Copy, [pp], [f_])
                P.dma(st_eng, s_f[tok, :], f_[:], reads=[f_], semt=f_)
        P.enabled = DBG_STOP is None or DBG_STOP >= 5
        P.dve(lambda e: e.tensor_scalar(rstdv[:], ssv[:], 1.0 / 512, EPS, ALU.mult, ALU.add), [ssv], [rstdv])
        P.act(rstdv[:], rstdv[:], AF.Sqrt, [rstdv], [rstdv])
        P.dve(lambda e: e.reciprocal(rstdv[:], rstdv[:]), [rstdv], [rstdv])
        P.dve(lambda e: e.scalar_tensor_tensor(vnb[:], vso[:], rstdv[:, 0:1], gsgu_bc[:], ALU.mult, ALU.mult),
              [vso, rstdv, gsgu_bc], [vnb])
        for g in range(2):
            P.mm(pX[:, g * 128:(g + 1) * 128], wsp_b[:, g, :], vnb[:, g * 128:(g + 1) * 128], True, True, [wsp_b, vnb], [pX])
        for g in range(2):
            P.act(mix[:, g * 128:(g + 1) * 128], pX[:, g * 128:(g + 1) * 128], AF.Identity, [pX, bsp_t], [mix],
                  bias=bsp_t[:, g:g + 1])
        P.dve(lambda e: e.tensor_tensor(mix[:], mix[:], usb[:], ALU.mult), [mix, usb], [mix])
        P.dve(lambda e: e.tensor_tensor(yb[:], mix[:], szb[:], ALU.mult), [mix, szb], [yb])
        for g in range(2):
            P.tr(pT[:, g * 128:(g + 1) * 128], yb[:, g * 128:(g + 1) * 128], ident_b[:], [yb, ident_b], [pT])
        yt_ = ybT[i % 2]
        P.dve(lambda e, yt_=yt_: e.tensor_copy(yt_[:], pT[:, 0:256].rearrange("p (a t) -> p a t", a=2)), [pT], [yt_])
        P.dma(st_eng, yT[512:768, :].rearrange("(a p) t -> p a t", p=128)[:, :, tok], yt_[:], reads=[yt_], semt=yt_)
        P.enabled = DBG_STOP is None or DBG_STOP >= 6
        h_ = hf[i % 2]
        mlstm_tile(i, 0, q_t, k_t, kt_t, va_t, h_)
        P.dma(st_eng, s_hf[tok, :], h_[:].rearrange("p h v -> p (h v)"), reads=[h_], semt=h_)
        P.enabled = True

    P.barrier()
    ph1.close()

    ph2 = P.phase()
    P.dve(lambda e: e.memset(Cm[:], 0.0), [], [Cm])
    P.dve(lambda e: e.memset(Cb[:], 0.0), [], [Cb])
    ktok2 = [P.sb("ktokb0", [128, 4, 64], BF16), P.sb("ktokb1", [128, 4, 64], BF16)]
    vaug2 = [P.sb("vaugb0", [128, 4, 129], BF16), P.sb("vaugb1", [128, 4, 129], BF16)]
    qT2 = [P.sb("qTb0", [128, 2, 128], BF16), P.sb("qTb1", [128, 2, 128], BF16)]
    kT2 = [P.sb("kTb0", [128, 2, 128], BF16), P.sb("kTb1", [128, 2, 128], BF16)]
    ga2 = [P.sb("gab0", [128, 512], BF16), P.sb("gab1", [128, 512], BF16)]
    hf2 = [P.sb("hfb0", [128, 4, 128], F32), P.sb("hfb1", [128, 4, 128], F32)]
    hb = P.sb("hb", [128, 4, 128], F32)
    hs = P.sb("hs", [128, 4, 128], F32)
    junk2 = P.sb("junk2", [128, 128], F32)
    ssh = P.sb("ssh", [128, 4], F32)
    ya = P.sb("ya", [128, 512], BF16)
    yaT = [P.sb("yaT0", [128, 4, 128], BF16), P.sb("yaT1", [128, 4, 128], BF16)]
    b_sb = P.sb("b_sb2", [128, 4], F32)
    a_sb = P.sb("a_sb2", [128, 4], F32)
    eb = P.sb("eb2", [128, 4], F32)
    bT = P.sb("bT2", [4, 128], F32)
    DT = P.sb("DT2", [128, 2, 128], F32)
    Sp = P.sb("Sp2", [128, 2, 128], BF16)
    tmpI = P.sb("tmpI2", [128, 129], F32)
    tot = P.sb("tot2", [128, 4, 129], F32)
    dn = P.sb("dn2", [128, 4], F32)
    dec = P.sb("dec2", [128, 4], F32)
    kw = P.sb("kw2", [128, 64], BF16)
    for v_ in vaug2:
        P.dve(lambda e, v_=v_: e.memset(v_[:], 1.0), [], [v_])

    order2 = [1, 0] + list(range(NT - 1, 1, -1))
    if DBG_NT is not None:
        order2 = order2[:DBG_NT]
    if 2 not in DBG_PH:
        order2 = []

    def load2(n):
        i = order2[n]
        tok = slice(i * 128, (i + 1) * 128)
        s = n % 2
        P.dma("sp", qT2[s][:], s_qT.rearrange("(a p) t -> p a t", p=128)[:, :, tok], writes=[qT2[s]], semt=qT2[s])
        P.dma("sp", kT2[s][:], s_kT.rearrange("(a p) t -> p a t", p=128)[:, :, tok], writes=[kT2[s]], semt=kT2[s])
        P.dma("sp", ktok2[s][:].rearrange("p h k -> p (h k)"), s_k[tok, :], writes=[ktok2[s]], semt=ktok2[s])
        P.dma("sp", vaug2[s][:, :, 0:128], s_v[tok, :].rearrange("p (h v) -> p h v", h=4), writes=[vaug2[s]], semt=vaug2[s])
        P.dma("sp", ga2[s][:], s_ga[tok, :], writes=[ga2[s]], semt=ga2[s])
        P.dma("sp", hf2[s][:].rearrange("p h v -> p (h v)"), s_hf[tok, :], writes=[hf2[s]], semt=hf2[s])

    if order2:
        load2(0)
    for n, i in enumerate(order2):
        if n + 1 < len(order2):
            load2(n + 1)
        s = n % 2
        tok = slice(i * 128, (i + 1) * 128)
        mlstm_tile(i, 1, qT2[s], kT2[s], ktok2[s], vaug2[s], hb)
        P.dve(lambda e, s=s: e.tensor_tensor(hs[:], hb[:], hf2[s][:], ALU.add), [hb, hf2[s]], [hs])
        for h in range(4):
            P.act(junk2[:], hs[:, h, :], AF.Square, [hs], [junk2, ssh], accum_out=ssh[:, h:h + 1])
        P.dve(lambda e: e.tensor_scalar(ssh[:], ssh[:], 1.0 / 128, EPS, ALU.mult, ALU.add), [ssh], [ssh])
        P.act(ssh[:], ssh[:], AF.Sqrt, [ssh], [ssh])
        P.dve(lambda e: e.reciprocal(ssh[:], ssh[:]), [ssh], [ssh])
        for h in range(4):
            P.dve(lambda e, h=h: e.scalar_tensor_tensor(hs[:, h, :], hs[:, h, :], ssh[:, h:h + 1],
                                                        ghn_bc[:, h * 128:(h + 1) * 128], ALU.mult, ALU.mult),
                  [hs, ssh, ghn_bc], [hs])
        P.dve(lambda e, s=s: e.tensor_tensor(ya[:], hs[:].rearrange("p h v -> p (h v)"), ga2[s][:], ALU.mult),
              [hs, ga2[s]], [ya])
        for h in range(4):
            P.tr(pT[:, h * 128:(h + 1) * 128], ya[:, h * 128:(h + 1) * 128], ident_b[:], [ya, ident_b], [pT])
        y_ = yaT[n % 2]
        P.act(y_[:], pT[:, 0:512].rearrange("p (a t) -> p a t", a=4), AF.Copy, [pT], [y_])
        P.dma(st_eng, yT[0:512, :].rearrange("(a p) t -> p a t", p=128)[:, :, tok], y_[:], reads=[y_], semt=y_)
    P.barrier()
    ph2.close()

    ph3 = P.phase()
    P.enabled = 3 in DBG_PH
    fr = P.sb("fr", [128, 64, 256], BF16)
    Ast = P.sb("Ast", [128, 128, 256], BF16)
    cs128f = P.sb("cs128f", [128, 256], F32)
    cs128 = P.sb("cs128", [128, 256], BF16)
    ccf = P.sb("ccf", [128, 2, 128], F32)
    wff = P.sb("wff", [128, 2, 128], F32)
    Mb = P.sb("Mb", [128, 2, 2, 128], BF16)
    bfn = P.sb("bfn", [128, 2], F32)
    ld(fr, s_f[T_CTX:, :].rearrange("(r col) ch -> r col ch", col=64))
    ld(cs128f, cs128_d)
    ld(ccf, cc128_d.rearrange("a d c -> d a c"))
    ld(wff, w_fno.rearrange("g d e -> d g e"))
    ld(bfn, b_fno.rearrange("g e -> e g"), allow_slow_non_contiguous=True)
    P.dve(lambda e: e.tensor_copy(cs128[:], cs128f[:]), [cs128f], [cs128])
    ccb = P.sb("ccb", [128, 2, 128], BF16)
    wfb = P.sb("wfb", [128, 2, 128], BF16)
    P.dve(lambda e: e.tensor_copy(ccb[:], ccf[:]), [ccf], [ccb])
    P.dve(lambda e: e.tensor_copy(wfb[:], wff[:]), [wff], [wfb])
    for g in range(2):
        for m in range(2):
            P.mm(pX[:, (2 * g + m) * 128:(2 * g + m + 1) * 128], ccb[:, m, :], wfb[:, g, :], True, True, [ccb, wfb], [pX])
    P.dve(lambda e: e.tensor_copy(Mb[:].rearrange("p g m e -> p (g m e)"), pX[:, 0:512]), [pX], [Mb])
    for cq in range(64):
        pp = pP[cq % 2]
        for j in range(4):
            ch = cq * 4 + j
            P.mm(pp[0:64, j * 128:(j + 1) * 128], fr[:, :, ch], cs128[:, 0:128], True, True, [fr, cs128], [pp])
            P.mm(pp[64:128, j * 128:(j + 1) * 128], fr[:, :, ch], cs128[:, 128:256], True, True, [fr, cs128], [pp])
        dst = Ast[:, :, cq * 4:cq * 4 + 4].rearrange("p k c -> p c k")
        src = pp[:, 0:512].rearrange("p (c k) -> p c k", c=4)
        if cq % 2 == 0:
            P.act(dst, src, AF.Copy, [pp], [Ast])
        else:
            P.dve(lambda e, dst=dst, src=src: e.tensor_copy(dst, src), [pp], [Ast])
    P.barrier()
    w3f = P.sb("w3f", [128, 1024], F32)
    w3 = P.sb("w3", [128, 2, 128, 64], BF16)
    for a in range(2):
        for q in range(8):
            ld(w3f, w3_d[a].rearrange("p k1 k2 -> p (k1 k2)")[:, q * 1024:(q + 1) * 1024])
            P.dve(lambda e, a=a, q=q: e.tensor_copy(w3[:, a, q * 16:(q + 1) * 16, :].rearrange("p a b -> p (a b)"), w3f[:]),
                  [w3f], [w3])
    ZT = P.sb("ZT", [128, 2, T_LAT], BF16)
    szc = [P.sb("szc0", [128, 512], BF16), P.sb("szc1", [128, 512], BF16)]
    ycs = P.sb("ycs", [128, 512], F32)
    ycT = [P.sb("ycT0", [128, 512], BF16), P.sb("ycT1", [128, 512], BF16)]
    fc = P.sb("fc", [128, 2, 256], BF16)
    c256f = P.sb("c256f", [128, 2, 2, 256], F32)
    c256 = P.sb("c256", [128, 2, 2, 256], BF16)
    ZC = P.sb("ZC", [128, 2, 256], BF16)
    ld(fc, s_f[0:T_CTX, :].rearrange("(c t) ch -> t c ch", t=128))
    ld(c256f, cs256_d.rearrange("a c t k -> t a c k"))
    P.dve(lambda e: e.tensor_copy(c256[:], c256f[:]), [c256f], [c256])
    for g in range(2):
        for a in range(2):
            for c in range(2):
                P.mm(pX[:, a * 256:(a + 1) * 256], fc[:, c, g * 128:(g + 1) * 128], c256[:, a, c, :], c == 0, c == 1,
                     [fc, c256], [pX])
        P.dve(lambda e: e.tensor_copy(ZC[:].rearrange("p a k -> p (a k)"), pX[:, 0:512]), [pX], [ZC])
        for a in range(2):
            P.mm(pF[:, 0:256], Mb[:, g, a, :], ZC[:, a, :], a == 0, a == 1, [Mb, ZC], [pF])
        P.act(ycs[:, 0:256], pF[:, 0:256], AF.Identity, [pF, bfn], [ycs], scale=float(1.0 / np.sqrt(256.0 * 128.0)),
              bias=bfn[:, g:g + 1])
        P.dma("sp", szc[g][:, 0:256], s_zcT[g * 128:(g + 1) * 128, 0:T_CTX], writes=[szc[g]], semt=szc[g])
        P.dve(lambda e, g=g: e.tensor_tensor(ycT[g][:, 0:256], ycs[:, 0:256], szc[g][:, 0:256], ALU.mult),
              [ycs, szc[g]], [ycT[g]])
        P.dma(st_eng, yT[768 + g * 128:768 + (g + 1) * 128, 0:T_CTX], ycT[g][:, 0:256], reads=[ycT[g]], semt=ycT[g])
    sc_lat = float(1.0 / np.sqrt(8192.0 * 128.0))
    for g in range(2):
        for a in range(2):
            for kq in range(16):
                pp = pP[kq % 2]
                for j in range(8):
                    k1 = kq * 8 + j
                    P.mm(pp[:, j * 64:(j + 1) * 64], Ast[:, k1, g * 128:(g + 1) * 128], w3[:, a, k1, :], True, True,
                         [Ast, w3], [pp])
                dst = ZT[:, a, :].rearrange("p (k2 k1) -> p k1 k2", k1=128)[:, kq * 8:kq * 8 + 8, :]
                src = pp[:, 0:512].rearrange("p (j k) -> p j k", j=8)
                if kq % 2 == 0:
                    P.act(dst, src, AF.Copy, [pp], [ZT])
                else:
                    P.dve(lambda e, dst=dst, src=src: e.tensor_copy(dst, src), [pp], [ZT])
        for tb in range(16):
            tk = slice(tb * 512, (tb + 1) * 512)
            tkd = slice(T_CTX + tb * 512, T_CTX + (tb + 1) * 512)
            s = tb % 2
            P.dma("sp", szc[s][:], s_zcT[g * 128:(g + 1) * 128, tkd], writes=[szc[s]], semt=szc[s])
            for a in range(2):
                P.mm(pF[:], Mb[:, g, a, :], ZT[:, a, tk], a == 0, a == 1, [Mb, ZT], [pF])
            P.act(ycs[:], pF[:], AF.Identity, [pF, bfn], [ycs], scale=sc_lat, bias=bfn[:, g:g + 1])
            P.dve(lambda e, s=s: e.tensor_tensor(ycT[s][:], ycs[:], szc[s][:], ALU.mult), [ycs, szc[s]], [ycT[s]])
            P.dma(st_eng, yT[768 + g * 128:768 + (g + 1) * 128, tkd], ycT[s][:], reads=[ycT[s]], semt=ycT[s])
    P.enabled = True
    P.barrier()
    ph3.close()
    cnt = P.emit()
    return nc


NTB = 33


def build_B():
    nc = bass.Bass("TRN2", target_bir_lowering=False)

    def din(name, shape, dt=F32):
        return nc.dram_tensor(name, list(shape), dt, kind="ExternalInput").ap()

    yTin = din("yTin", [2048, NTB * 128], BF16)
    xb = din("xb", [NTB * 128, D])
    cvec = din("cvec", [2, D])
    w_mod = din("w_mod", [D, 1024])
    b_mod = din("b_mod", [1024])
    g_post = din("g_post", [D])
    w_out = din("w_out", [2048, D])
    xo = nc.dram_tensor("xo", [NTB * 128, D], F32, kind="ExternalOutput").ap()

    P = Prog(nc)
    wob = P.sb("wob", [128, 16, D], BF16)
    G = P.sb("G", [128, 2, D], F32)
    gpo = P.sb("gpo", [128, D], F32)
    bmo = P.sb("bmo", [128, D], F32)
    pY = [P.ps("pY0"), P.ps("pY1"), P.ps("pY2"), P.ps("pY3")]
    pG = P.ps("pG")

    ph0 = P.phase()
    wmod = P.sb("wmod", [128, 8, 1024], F32)
    cT = P.sb("cT", [128, 8, 2], F32)
    scT = P.sb("scT", [128, 8, 2], F32)
    scB = P.sb("scB", [128, 8, 128], BF16)
    wmodb = P.sb("wmodb", [128, 8, 1024], BF16)
    wst = [P.sb("wst0", [128, D], F32), P.sb("wst1", [128, D], F32)]
    P.dma("sp", wmod[:], w_mod.rearrange("(c p) n -> p c n", p=128), writes=[wmod], semt=wmod)
    for m in range(2):
        P.dma("sp", cT[:, :, m], cvec[m].rearrange("(c p) -> p c", p=128), writes=[cT], semt=cT, allow_slow_non_contiguous=True)
    P.dma("sp", gpo[:], g_post.partition_broadcast(128), writes=[gpo], semt=gpo)
    P.dma("sp", bmo[:], b_mod.partition_broadcast(128), writes=[bmo], semt=bmo)
    P.act(scT[:], cT[:], AF.Silu, [cT], [scT])
    for c in range(8):
        if c % 2 == 0:
            P.dve(lambda e: e.tensor_copy(wmodb[:, c, :], wmod[:, c, :]), [wmod], [wmodb])
        else:
            P.act(wmodb[:, c, :], wmod[:, c, :], AF.Copy, [wmod], [wmodb])
    for m in range(2):
        P.dve(lambda e, m=m: e.tensor_copy(scB[:], scT[:, :, m:m + 1].to_broadcast([128, 8, 128])), [scT], [scB])
        for jb in range(2):
            for c in range(8):
                P.mm(pG[:], scB[:, c, :], wmodb[:, c, jb * 512:(jb + 1) * 512], c == 0, c == 7, [scB, wmodb], [pG])
            P.dve(lambda e, m=m, jb=jb: e.tensor_tensor(G[:, m, jb * 512:(jb + 1) * 512], pG[:], bmo[:, jb * 512:(jb + 1) * 512],
                                                        ALU.add), [pG, bmo], [G])
        P.dve(lambda e, m=m: e.tensor_tensor(G[:, m, :], G[:, m, :], gpo[:], ALU.mult), [G, gpo], [G])
    for c in range(16):
        P.dma("sp", wst[c % 2][:], w_out[c * 128:(c + 1) * 128, :], writes=[wst[c % 2]], semt=wst[c % 2])
        if c % 2 == 0:
            P.dve(lambda e, c=c: e.tensor_copy(wob[:, c, :], wst[c % 2][:]), [wst[c % 2]], [wob])
        else:
            P.act(wob[:, c, :], wst[c % 2][:], AF.Copy, [wst[c % 2]], [wob])
    P.barrier()
    ph0.close()

    ph1 = P.phase()
    ytl = [P.sb("ytl0", [128, 16, 128], BF16), P.sb("ytl1", [128, 16, 128], BF16)]
    xt = [P.sb("xt0", [128, D], F32), P.sb("xt1", [128, D], F32)]
    ot = [P.sb("ot0", [128, D], F32), P.sb("ot1", [128, D], F32)]
    junk = P.sb("junk", [128, 512], F32)
    ss2 = P.sb("ss2", [128, 2], F32)
    rs = P.sb("rs", [128, 1], F32)

    def loadB(i):
        tok = slice(i * 128, (i + 1) * 128)
        P.dma("sp", ytl[i % 2][:], yTin.rearrange("(c p) t -> p c t", p=128)[:, :, tok], writes=[ytl[i % 2]], semt=ytl[i % 2])
        P.dma("sp", xt[i % 2][:], xb[tok, :], writes=[xt[i % 2]], semt=xt[i % 2])

    loadB(0)
    for i in range(NTB):
        if i + 1 < NTB:
            loadB(i + 1)
        m = 0 if i == 0 else 1
        tok = slice(i * 128, (i + 1) * 128)
        y_, x_, o_ = ytl[i % 2], xt[i % 2], ot[i % 2]
        pa, pb = pY[2 * (i % 2)], pY[2 * (i % 2) + 1]
        for jb, pp in enumerate((pa, pb)):
            for c in range(16):
                P.mm(pp[:], y_[:, c, :], wob[:, c, jb * 512:(jb + 1) * 512], c == 0, c == 15, [y_, wob], [pp])
            P.act(junk[:], pp[:], AF.Square, [pp], [junk, ss2], accum_out=ss2[:, jb:jb + 1])
        P.dve(lambda e: e.tensor_tensor(rs[:], ss2[:, 0:1], ss2[:, 1:2], ALU.add), [ss2], [rs])
        P.dve(lambda e: e.tensor_scalar(rs[:], rs[:], 1.0 / D, EPS, ALU.mult, ALU.add), [rs], [rs])
        P.act(rs[:], rs[:], AF.Sqrt, [rs], [rs])
        P.dve(lambda e: e.reciprocal(rs[:], rs[:]), [rs], [rs])
        for jb, pp in enumerate((pa, pb)):
            cs = slice(jb * 512, (jb + 1) * 512)
            P.dve(lambda e, pp=pp, cs=cs, o_=o_, m=m: e.scalar_tensor_tensor(o_[:, cs], pp[:], rs[:, 0:1], G[:, m, cs],
                                                                           ALU.mult, ALU.mult), [pp, rs, G], [o_])
        P.dve(lambda e, o_=o_, x_=x_: e.tensor_tensor(o_[:], o_[:], x_[:], ALU.add), [o_, x_], [o_])
        P.dma("pool", xo[tok, :], o_[:], reads=[o_], semt=o_)
    P.barrier()
    ph1.close()
    P.emit()
    return nc


_CACHE = {}


def _pack_w_in(w_in_l, hh):
    A_Q, A_K, A_V, A_G = 0, 512, 1024, 2048
    R = 2080
    O_, ZA_, U_, VS_, ZB_, F_, ZC_ = R, R + 1024, R + 2048, R + 2560, R + 3072, R + 3584, R + 4096
    hs = slice(hh * 4, hh * 4 + 4)

    def heads(base, d):
        return np.arange(base + hh * 4 * d, base + (hh * 4 + 4) * d)

    def groups(base):
        return np.arange(base + hh * 256, base + hh * 256 + 256)
    gate = np.concatenate([A_G + t * 8 + np.arange(hh * 4, hh * 4 + 4) for t in range(4)])
    vs_own = groups(VS_)
    vs_oth = np.arange(VS_ + (1 - hh) * 256, VS_ + (1 - hh) * 256 + 256)
    cols = np.concatenate([heads(A_Q, 64), heads(A_K, 64), groups(ZC_), heads(A_K, 64), gate, heads(A_V, 128),
                           heads(O_, 128), heads(ZA_, 128), groups(U_), groups(ZB_), vs_own, vs_oth, groups(F_)])
    assert cols.shape[0] == NCOL
    return np.ascontiguousarray(w_in_l[:, cols]), gate - A_G


def kernel(x, c, ctx, c_ctx, w_mod, b_mod, g_pre, g_post, w_in, b_gate, g_hnorm, g_sgu, w_sp, b_sp, w_fno, b_fno, w_out):
    f32 = np.float32
    x = np.asarray(x, f32)
    xc = np.asarray(ctx, f32)
    c = np.asarray(c, f32)
    c_ctx = np.asarray(c_ctx, f32)
    if "A" not in _CACHE:
        _CACHE["A"] = build_A()
        _CACHE["B"] = build_B()
        _CACHE["consts"] = _consts()
    ncA, ncB, cst = _CACHE["A"], _CACHE["B"], _CACHE["consts"]
    B = x.shape[0]
    for l in range(2):
        wm, bm = np.asarray(w_mod[l], f32), np.asarray(b_mod[l], f32)
        in_maps = []
        for core in range(8):
            b, hh = core // 2, core % 2
            wc, gidx = _pack_w_in(np.asarray(w_in[l], f32), hh)
            gs = np.asarray(g_sgu[l], f32)
            m = dict(cst)
            m.update({
                "xin": np.ascontiguousarray(np.concatenate([xc[b], x[b]], axis=0)),
                "cvec": np.ascontiguousarray(np.stack([c_ctx, c[b]])),
                "w_mod": np.ascontiguousarray(wm[:, 0:2048]),
                "b_mod": np.ascontiguousarray(bm[0:2048]),
                "g_pre": np.asarray(g_pre[l], f32),
                "w_in": wc,
                "b_gate": np.ascontiguousarray(np.asarray(b_gate[l], f32)[gidx]),
                "g_hn": np.ascontiguousarray(np.asarray(g_hnorm[l], f32)[hh * 512:(hh + 1) * 512]),
                "g_sgu": np.ascontiguousarray(gs[hh * 256:(hh + 1) * 256]),
                "w_spT": np.ascontiguousarray(np.asarray(w_sp[l], f32)[hh * 2:hh * 2 + 2].transpose(0, 2, 1)),
                "b_sp": np.ascontiguousarray(np.asarray(b_sp[l], f32)[hh * 2:hh * 2 + 2]),
                "w_fno": np.ascontiguousarray(np.asarray(w_fno[l], f32)[hh * 2:hh * 2 + 2]),
                "b_fno": np.ascontiguousarray(np.asarray(b_fno[l], f32)[hh * 2:hh * 2 + 2]),
            })
            in_maps.append(m)
        resA = run_bass_kernel_spmd(ncA, in_maps, core_ids=list(range(8)))
        yTs = [np.asarray(r["yT"]) for r in resA.results]
        _CACHE["last_yT"] = yTs
        wo = np.asarray(w_out[l], f32)
        rows = []
        for hh in range(2):
            rows += [np.arange(hh * 512, hh * 512 + 512), 1024 + np.arange(hh * 256, hh * 256 + 256),
                     1536 + np.arange(hh * 256, hh * 256 + 256)]
        wo_p = np.ascontiguousarray(wo[np.concatenate(rows)])
        in_maps = []
        for core in range(8):
            b, half = core // 2, core % 2
            yfull = np.concatenate([yTs[2 * b], yTs[2 * b + 1]], axis=0)
            tcols = np.concatenate([np.arange(half * 128, half * 128 + 128),
                                    T_CTX + np.arange(half * 4096, half * 4096 + 4096)])
            xrows = np.concatenate([xc[b, half * 128:(half + 1) * 128], x[b, half * 4096:(half + 1) * 4096]], axis=0)
            in_maps.append({
                "yTin": np.ascontiguousarray(yfull[:, tcols]),
                "xb": np.ascontiguousarray(xrows),
                "cvec": np.ascontiguousarray(np.stack([c_ctx, c[b]])),
                "w_mod": np.ascontiguousarray(wm[:, 2048:3072]),
                "b_mod": np.ascontiguousarray(bm[2048:3072]),
                "g_post": np.asarray(g_post[l], f32),
                "w_out": wo_p,
            })
        resB = run_bass_kernel_spmd(ncB, in_maps, core_ids=list(range(8)))
        xn_ = np.empty_like(x)
        xcn = np.empty_like(xc)
        for core in range(8):
            b, half = core // 2, core % 2
            o = np.asarray(resB.results[core]["xo"])
            xcn[b, half * 128:(half + 1) * 128] = o[0:128]
            xn_[b, half * 4096:(half + 1) * 4096] = o[128:]
        x, xc = xn_, xcn
    return x.astype(np.float32)
```
